# Optimizing a Trainium2 kernel written in Bass

```python
import math
import jax, jax.numpy as jnp
from jax import lax
import numpy as np

D_MODEL = 2048
BATCH = 4
SEQ = 4096
DEPTH = 2

GRID_W = 64
CTX_LEN = 256
N_MIXERS = 2
N_RWKV_LAYERS = (DEPTH + N_MIXERS - 1) // N_MIXERS
N_MLA_LAYERS = DEPTH // N_MIXERS
RMS_EPS = 1e-6
RWKV_HEAD = 64
RWKV_HEADS = D_MODEL // RWKV_HEAD
DECAY_LORA = 96
ICLR_LORA = 96
GATE_LORA = 256
GN_EPS = 64e-5
MLA_HEADS = 16
Q_LORA = 512
KV_LORA = 512
QK_NOPE = 128
QK_ROPE = 64
V_HEAD = 128
MLA_SCALE = (QK_NOPE + QK_ROPE) ** -0.5
ROPE_THETA = 10000.0
ROPE_PAIRS_PER_AXIS = QK_ROPE // 4
Q_BLOCK = 128
D_FF = 5632
CONV_W = 3

kernel_name = "hybrid_rwkv7_mla_convffn_dit"


def rmsnorm(x, g):
    xf = x.astype(jnp.float32)
    y = xf * lax.rsqrt(jnp.mean(xf * xf, axis=-1, keepdims=True) + RMS_EPS)
    return y.astype(x.dtype) * g


def modulate(h, shift, scale):
    return h * (1.0 + scale) + shift


def shift_prev_next(h):
    z = jnp.zeros_like(h[:, :1])
    prev = jnp.concatenate([z, h[:, :-1]], axis=1)
    nxt = jnp.concatenate([h[:, 1:], z], axis=1)
    return prev, nxt


def l2_normalize(t):
    tf = t.astype(jnp.float32)
    n = jnp.sqrt(jnp.sum(tf * tf, axis=-1, keepdims=True))
    return (tf / jnp.maximum(n, 1e-12)).astype(t.dtype)


def rwkv_project(h, mu, wr, wk, wv, w0, w1, w2, a0, a1, a2, g1, g2, k_k, k_a):
    B, L, _ = h.shape
    heads = lambda t: t.reshape(B, L, RWKV_HEADS, RWKV_HEAD)
    prev, nxt = shift_prev_next(h)
    xx = 0.5 * (prev + nxt) - h
    xr, xw, xk, xv, xa, xg = (h + xx * mu[j] for j in range(6))
    r = heads(xr @ wr)
    k = xk @ wk
    v = heads(xv @ wv)
    g = jax.nn.sigmoid(xg @ g1) @ g2
    kk = l2_normalize(heads(k * k_k))
    dirs = []
    for d in range(2):
        w_log = -jax.nn.softplus(-(w0[d] + jnp.tanh(xw @ w1[d]) @ w2[d])) - 0.5
        decay = jnp.exp(-jnp.exp(w_log.astype(jnp.float32)))
        a = jax.nn.sigmoid(a0[d] + (xa @ a1[d]) @ a2[d])
        kd = heads(k * (1.0 + (a - 1.0) * k_a))
        dirs.append((heads(decay), kd, kk * heads(a)))
    return r, v, g, kk, dirs


def rwkv7_scan(S0, r, w, k, v, a, b, reverse):
    xs = tuple(jnp.moveaxis(t.astype(jnp.float32), 1, 0) for t in (r, w, k, v, a, b))

    def step(S, inp):
        rt, wt, kt, vt, at, bt = inp
        sa = jnp.einsum('bhvk,bhk->bhv', S, at)
        S = S * wt[:, :, None, :] + sa[..., :, None] * bt[:, :, None, :] + vt[..., :, None] * kt[:, :, None, :]
        return S, jnp.einsum('bhvk,bhk->bhv', S, rt)

    S, ys = lax.scan(step, S0, xs, reverse=reverse)
    return S, jnp.moveaxis(ys, 0, 1)


def rwkv_readout(y, r, ks, v, g, r_k, gn_w, gn_b, wo):
    B, L, H, N = r.shape
    mean = jnp.mean(y, axis=-1, keepdims=True)
    var = jnp.mean((y - mean) ** 2, axis=-1, keepdims=True)
    yn = ((y - mean) * lax.rsqrt(var + GN_EPS)).astype(r.dtype).reshape(B, L, H * N) * gn_w + gn_b
    bonus = (jnp.sum(r * ks[0] * r_k, axis=-1, keepdims=True)
             + jnp.sum(r * ks[1] * r_k, axis=-1, keepdims=True)) * v
    return ((yn + bonus.reshape(B, L, H * N)) * g) @ wo


def rwkv_mixer(h_ctx, h_lat, need_ctx, mu, wr, wk, wv, wo, w0, w1, w2, a0, a1, a2, g1, g2,
               k_k, k_a, r_k, gn_w, gn_b):
    proj = lambda h: rwkv_project(h, mu, wr, wk, wv, w0, w1, w2, a0, a1, a2, g1, g2, k_k, k_a)
    rc, vc, gc, kkc, dc = proj(h_ctx)
    rl, vl, gl, kkl, dl = proj(h_lat)
    B = h_lat.shape[0]
    S0 = jnp.zeros((B, RWKV_HEADS, RWKV_HEAD, RWKV_HEAD), jnp.float32)
    ys_c, ys_l = [], []
    for d, rev in enumerate((False, True)):
        wc, kc, bc = dc[d]
        wl, kl, bl = dl[d]
        S_ctx, yc = rwkv7_scan(S0, rc, wc, kc, vc, -kkc, bc, rev)
        _, yl = rwkv7_scan(S_ctx, rl, wl, kl, vl, -kkl, bl, rev)
        ys_c.append(yc)
        ys_l.append(yl)
    out_l = rwkv_readout(ys_l[0] + ys_l[1], rl, (dl[0][1], dl[1][1]), vl, gl, r_k, gn_w, gn_b, wo)
    out_c = None
    if need_ctx:
        out_c = rwkv_readout(ys_c[0] + ys_c[1], rc, (dc[0][1], dc[1][1]), vc, gc, r_k, gn_w, gn_b, wo)
    return out_l, out_c


def rope_half(x, cos, sin):
    x1, x2 = jnp.split(x, 2, axis=-1)
    return jnp.concatenate([x1 * cos - x2 * sin, x2 * cos + x1 * sin], axis=-1)


def axial_rope(x, cos_r, sin_r, cos_c, sin_c):
    xr, xc = jnp.split(x, 2, axis=-1)
    return jnp.concatenate([rope_half(xr, cos_r, sin_r), rope_half(xc, cos_c, sin_c)], axis=-1)


def mla_project(h, wdown, qnorm, kvnorm, wuq, wukv, rope):
    B, L, _ = h.shape
    c_q, c_kv, k_rope = jnp.split(h @ wdown, [Q_LORA, Q_LORA + KV_LORA], axis=-1)
    q = (rmsnorm(c_q, qnorm) @ wuq).reshape(B, L, MLA_HEADS, QK_NOPE + QK_ROPE)
    kv = (rmsnorm(c_kv, kvnorm) @ wukv).reshape(B, L, MLA_HEADS, QK_NOPE + V_HEAD)
    q_nope, q_rope = jnp.split(q, [QK_NOPE], axis=-1)
    k_nope, v = jnp.split(kv, [QK_NOPE], axis=-1)
    if rope is not None:
        q_rope = axial_rope(q_rope, *(t[:, None, :] for t in rope))
        k_rope = axial_rope(k_rope, *rope)
    return q_nope, q_rope, k_nope, k_rope, v


def mla_attend(q_nope, q_rope, k_nope, k_rope, v):
    s = (jnp.einsum('bqhd,bkhd->bhqk', q_nope, k_nope)
         + jnp.einsum('bqhr,bkr->bhqk', q_rope, k_rope))
    p = jax.nn.softmax(s.astype(jnp.float32) * MLA_SCALE, axis=-1).astype(v.dtype)
    return jnp.einsum('bhqk,bkhd->bqhd', p, v)


def mla_mixer(h_ctx, h_lat, need_ctx, rope, wdown, qnorm, kvnorm, wuq, wukv, wo):
    qn_c, qr_c, kn_c, kr_c, v_c = mla_project(h_ctx, wdown, qnorm, kvnorm, wuq, wukv, None)
    qn_l, qr_l, kn_l, kr_l, v_l = mla_project(h_lat, wdown, qnorm, kvnorm, wuq, wukv, rope)
    kn = jnp.concatenate([kn_c, kn_l], axis=1)
    kr = jnp.concatenate([kr_c, kr_l], axis=1)
    v = jnp.concatenate([v_c, v_l], axis=1)
    B, L = h_lat.shape[0], h_lat.shape[1]
    nb = L // Q_BLOCK
    to_blocks = lambda t: jnp.moveaxis(t.reshape(B, nb, Q_BLOCK, *t.shape[2:]), 1, 0)
    o = lax.map(lambda qb: mla_attend(qb[0], qb[1], kn, kr, v), (to_blocks(qn_l), to_blocks(qr_l)))
    o = jnp.moveaxis(o, 0, 1).reshape(B, L, MLA_HEADS * V_HEAD)
    out_l = o @ wo
    out_c = None
    if need_ctx:
        oc = mla_attend(qn_c, qr_c, kn_c, kr_c, v_c)
        out_c = oc.reshape(B, h_ctx.shape[1], MLA_HEADS * V_HEAD) @ wo
    return out_l, out_c


def conv_ffn(h, wup, conv_w, conv_b, wdown):
    gate, val = jnp.split(h @ wup, 2, axis=-1)
    prev, nxt = shift_prev_next(gate)
    gate = prev * conv_w[0] + gate * conv_w[1] + nxt * conv_w[2] + conv_b
    return (jax.nn.silu(gate) * val) @ wdown


def setup_inputs(seed: int = 0) -> dict:
    key = jax.random.key(seed)
    ks = iter(jax.random.split(key, 48))
    f32 = jnp.float32
    D, H, N, NA, NB = D_MODEL, RWKV_HEADS, RWKV_HEAD, N_RWKV_LAYERS, N_MLA_LAYERS

    def nrm(shape, scale):
        return jax.random.normal(next(ks), shape, f32) * scale

    def gain(shape):
        return 1.0 + nrm(shape, 0.02)

    decay_base = -6.0 + 5.0 * jnp.linspace(0.0, 1.0, D, dtype=f32) ** 1.5
    return {
        "x": nrm((BATCH, SEQ, D), 1.0),
        "c": nrm((BATCH, D), 1.0),
        "ctx": nrm((BATCH, CTX_LEN, D), 1.0),
        "c_ctx": nrm((D,), 1.0),
        "ada_w": nrm((DEPTH, D, 6 * D), 0.5 * D ** -0.5),
        "ada_b": nrm((DEPTH, 6 * D), 0.02),
        "norm_g": gain((DEPTH, 2, D)),
        "final_g": gain((D,)),
        "rk_mu": jax.random.uniform(next(ks), (NA, 6, D), f32),
        "rk_wr": nrm((NA, D, D), D ** -0.5),
        "rk_wk": nrm((NA, D, D), D ** -0.5),
        "rk_wv": nrm((NA, D, D), D ** -0.5),
        "rk_wo": nrm((NA, D, D), D ** -0.5),
        "rk_w0": decay_base + nrm((NA, 2, D), 0.1),
        "rk_w1": nrm((NA, 2, D, DECAY_LORA), D ** -0.5),
        "rk_w2": nrm((NA, 2, DECAY_LORA, D), 0.1 * DECAY_LORA ** -0.5),
        "rk_a0": nrm((NA, 2, D), 0.1),
        "rk_a1": nrm((NA, 2, D, ICLR_LORA), D ** -0.5),
        "rk_a2": nrm((NA, 2, ICLR_LORA, D), 0.1 * ICLR_LORA ** -0.5),
        "rk_g1": nrm((NA, D, GATE_LORA), D ** -0.5),
        "rk_g2": nrm((NA, GATE_LORA, D), GATE_LORA ** -0.5),
        "rk_kk": 0.85 + nrm((NA, D), 0.02),
        "rk_ka": gain((NA, D)),
        "rk_rk": nrm((NA, H, N), 0.1),
        "rk_gn_w": gain((NA, D)),
        "rk_gn_b": nrm((NA, D), 0.02),
        "ml_wdown": nrm((NB, D, Q_LORA + KV_LORA + QK_ROPE), D ** -0.5),
        "ml_qnorm": gain((NB, Q_LORA)),
        "ml_kvnorm": gain((NB, KV_LORA)),
        "ml_wuq": nrm((NB, Q_LORA, MLA_HEADS * (QK_NOPE + QK_ROPE)), Q_LORA ** -0.5),
        "ml_wukv": nrm((NB, KV_LORA, MLA_HEADS * (QK_NOPE + V_HEAD)), KV_LORA ** -0.5),
        "ml_wo": nrm((NB, MLA_HEADS * V_HEAD, D), (MLA_HEADS * V_HEAD) ** -0.5),
        "ff_wup": nrm((DEPTH, D, 2 * D_FF), D ** -0.5),
        "ff_conv": nrm((DEPTH, CONV_W, D_FF), 0.5),
        "ff_convb": nrm((DEPTH, D_FF), 0.02),
        "ff_wdown": nrm((DEPTH, D_FF, D), D_FF ** -0.5),
    }


def reference(x, c, ctx, c_ctx, ada_w, ada_b, norm_g, final_g,
              rk_mu, rk_wr, rk_wk, rk_wv, rk_wo, rk_w0, rk_w1, rk_w2, rk_a0, rk_a1, rk_a2,
              rk_g1, rk_g2, rk_kk, rk_ka, rk_rk, rk_gn_w, rk_gn_b,
              ml_wdown, ml_qnorm, ml_kvnorm, ml_wuq, ml_wukv, ml_wo,
              ff_wup, ff_conv, ff_convb, ff_wdown):
    n_lat = x.shape[1]
    rows = n_lat // GRID_W
    row = jnp.broadcast_to(jnp.arange(rows)[:, None], (rows, GRID_W)).reshape(-1)
    col = jnp.broadcast_to(jnp.arange(GRID_W)[None, :], (rows, GRID_W)).reshape(-1)
    inv_freq = jnp.float32(ROPE_THETA) ** (-jnp.arange(ROPE_PAIRS_PER_AXIS, dtype=jnp.float32) / ROPE_PAIRS_PER_AXIS)
    ang_r = row.astype(jnp.float32)[:, None] * inv_freq
    ang_c = col.astype(jnp.float32)[:, None] * inv_freq
    rope = (jnp.cos(ang_r).astype(x.dtype), jnp.sin(ang_r).astype(x.dtype),
            jnp.cos(ang_c).astype(x.dtype), jnp.sin(ang_c).astype(x.dtype))

    s_ctx = ctx
    for i in range(DEPTH):
        last = i == DEPTH - 1
        m_lat = jax.nn.silu(c) @ ada_w[i] + ada_b[i]
        m_ctx = jax.nn.silu(c_ctx) @ ada_w[i] + ada_b[i]
        sh1, sc1, gt1, sh2, sc2, gt2 = (t[:, None, :] for t in jnp.split(m_lat, 6, axis=-1))
        csh1, csc1, cgt1, csh2, csc2, cgt2 = jnp.split(m_ctx, 6, axis=-1)

        h_lat = modulate(rmsnorm(x, norm_g[i, 0]), sh1, sc1)
        h_ctx = modulate(rmsnorm(s_ctx, norm_g[i, 0]), csh1, csc1)
        j = i // N_MIXERS
        if i % N_MIXERS == 0:
            out_l, out_c = rwkv_mixer(h_ctx, h_lat, not last, rk_mu[j], rk_wr[j], rk_wk[j], rk_wv[j], rk_wo[j],
                                      rk_w0[j], rk_w1[j], rk_w2[j], rk_a0[j], rk_a1[j], rk_a2[j],
                                      rk_g1[j], rk_g2[j], rk_kk[j], rk_ka[j], rk_rk[j], rk_gn_w[j], rk_gn_b[j])
        else:
            out_l, out_c = mla_mixer(h_ctx, h_lat, not last, rope, ml_wdown[j], ml_qnorm[j], ml_kvnorm[j],
                                     ml_wuq[j], ml_wukv[j], ml_wo[j])
        x = x + gt1 * out_l
        hf_lat = modulate(rmsnorm(x, norm_g[i, 1]), sh2, sc2)
        x = x + gt2 * conv_ffn(hf_lat, ff_wup[i], ff_conv[i], ff_convb[i], ff_wdown[i])
        if not last:
            s_ctx = s_ctx + cgt1 * out_c
            hf_ctx = modulate(rmsnorm(s_ctx, norm_g[i, 1]), csh2, csc2)
            s_ctx = s_ctx + cgt2 * conv_ffn(hf_ctx, ff_wup[i], ff_conv[i], ff_convb[i], ff_wdown[i])
    return rmsnorm(x, final_g)
```

```python
import numpy as np
import concourse.bass as bass
import concourse.mybir as mybir
from concourse.bass_utils import run_bass_kernel_spmd

F32 = mybir.dt.float32
BF16 = mybir.dt.bfloat16
AF = mybir.ActivationFunctionType
ALU = mybir.AluOpType
AX = mybir.AxisListType

SAME_ENGINE_SYNC = True
DMA_ROT = 6


def _region(ap):
    t = ap.tensor
    es = int(mybir.dt.size(ap.dtype))
    dims = [(int(a) * es, int(b)) for a, b in ap.ap]
    off = int(ap.offset) * es
    if type(t).__name__.startswith("DRam") or type(t).__name__.startswith("Dram") or getattr(ap, "space", None) is not None and "DRAM" in str(ap.space).upper():
        ext = 0
        for st, cn in dims:
            ext += (int(cn) - 1) * abs(int(st))
        return (t.name, 0, 1, off, off + ext + es)
    pstep, pcnt = int(dims[0][0]), int(dims[0][1])
    if pstep <= 0:
        pstep = 1 << 40
    p0 = off // pstep
    f0 = off % pstep
    ext = 0
    for st, cn in dims[1:]:
        ext += (int(cn) - 1) * abs(int(st))
    return (t.name, p0, p0 + pcnt, f0, f0 + ext + es)


class Op:
    __slots__ = ("eng", "fn", "rd", "wr", "dma", "deps", "marked", "sem", "val", "idx", "prewait")


class Prog:
    ENG = ("pe", "dve", "act", "pool", "sp")

    def __init__(self):
        self.nc = bass.Bass("TRN2", target_bir_lowering=False)
        self.ops = []
        self.recs = {}
        self.ctx = []
        self.out_names = set()
        self.pending = {}
        self.last_op = {e: None for e in self.ENG}
        self.last_dma = {q: [] for q in ("sp", "act", "pool")}
        self.h = {"pe": self.nc.tensor, "dve": self.nc.vector, "act": self.nc.scalar,
                  "pool": self.nc.gpsimd, "sp": self.nc.sync}

    def dram(self, name, shape, dtype=F32, kind="ExternalInput"):
        t = self.nc.dram_tensor(name, list(shape), dtype, kind=kind)
        if kind == "ExternalOutput":
            self.out_names.add(name)
        return t.ap()

    def sb(self, name, shape, dtype=F32):
        c = self.nc.sbuf_tensor(name, list(shape), dtype)
        t = c.__enter__()
        self.ctx.append(c)
        return t

    def ps(self, name, shape, dtype=F32):
        c = self.nc.psum_tensor(name, list(shape), dtype)
        t = c.__enter__()
        self.ctx.append(c)
        return t

    def op(self, eng, fn, rd, wr, dma=False):
        o = Op()
        o.eng, o.fn, o.dma = eng, fn, dma
        o.rd = [_region(a) for a in rd]
        o.wr = [_region(a) for a in wr]
        o.idx = len(self.ops)
        o.marked = dma
        deps = set()
        for r in o.rd:
            lst = self.recs.get(r[0])
            if lst:
                for q in lst:
                    if q[5] and q[0] < r[2] and r[1] < q[1] and q[2] < r[4] and r[3] < q[3]:
                        deps.add(q[4])
        for r in o.wr:
            lst = self.recs.get(r[0])
            if lst:
                keep = []
                for q in lst:
                    if q[0] < r[2] and r[1] < q[1] and q[2] < r[4] and r[3] < q[3]:
                        deps.add(q[4])
                        if r[1] <= q[0] and q[1] <= r[2] and r[3] <= q[2] and q[3] <= r[4]:
                            continue
                    keep.append(q)
                self.recs[r[0]] = keep
        for r in o.rd:
            lst = self.recs.setdefault(r[0], [])
            if not dma:
                for q in lst:
                    if (not q[5]) and q[0] == r[1] and q[1] == r[2] and q[2] == r[3] and q[3] == r[4] \
                            and q[6] == eng:
                        q[4] = o.idx
                        break
                else:
                    lst.append([r[1], r[2], r[3], r[4], o.idx, False, eng])
            else:
                lst.append([r[1], r[2], r[3], r[4], o.idx, False, "dma"])
        for r in o.wr:
            self.recs.setdefault(r[0], []).append([r[1], r[2], r[3], r[4], o.idx, True, eng])
        fd = []
        for d in deps:
            p = self.ops[d]
            if (not p.dma) and (not dma) and p.eng == eng:
                if eng == "pe" or not SAME_ENGINE_SYNC:
                    continue
            if (not p.dma) and dma and p.eng == eng and False:
                continue
            p.marked = True
            fd.append(d)
        pend = self.pending.pop(eng, None)
        if pend:
            for d in pend:
                self.ops[d].marked = True
                fd.append(d)
        o.deps = fd
        self.ops.append(o)
        if dma:
            self.last_dma[eng].append(o.idx)
            if len(self.last_dma[eng]) > DMA_ROT:
                self.last_dma[eng].pop(0)
        else:
            self.last_op[eng] = o.idx
        return o

    def barrier(self):
        allp = [v for v in self.last_op.values() if v is not None]
        for q in self.last_dma.values():
            allp.extend(q)
        for e in self.ENG:
            self.pending[e] = list(allp)
        self.recs = {}

    def release(self, mark):
        while len(self.ctx) > mark:
            c = self.ctx.pop()
            c.__exit__(None, None, None)

    def mm(self, out, lhsT, rhs, start=True, stop=True):
        nc = self.nc
        return self.op("pe", lambda: nc.tensor.matmul(out, lhsT, rhs, start=start, stop=stop),
                       [lhsT, rhs], [out])

    def tr(self, out, in_, ident):
        nc = self.nc
        return self.op("pe", lambda: nc.tensor.transpose(out, in_, ident), [in_, ident], [out])

    def actv(self, out, in_, func, bias=None, scale=None, accum_out=None, eng="act"):
        nc = self.nc
        kw = {}
        rd = [in_]
        wr = [out]
        if bias is not None:
            kw["bias"] = bias
            if not isinstance(bias, (int, float)):
                rd.append(bias)
        if scale is not None:
            kw["scale"] = scale
            if not isinstance(scale, (int, float)):
                rd.append(scale)
        if accum_out is not None:
            kw["accum_out"] = accum_out
            wr.append(accum_out)
        return self.op("act", lambda: nc.scalar.activation(out=out, in_=in_, func=func, **kw), rd, wr)

    def tt(self, out, in0, in1, op, eng="dve"):
        h = self.h[eng]
        return self.op(eng, lambda: h.tensor_tensor(out=out, in0=in0, in1=in1, op=op), [in0, in1], [out])

    def ts(self, out, in0, s1, op0, s2=None, op1=None, eng="dve", accum_out=None):
        h = self.h[eng]
        rd = [in0]
        wr = [out]
        if not isinstance(s1, (int, float)):
            rd.append(s1)
        if s2 is not None and not isinstance(s2, (int, float)):
            rd.append(s2)
        kw = {}
        if op1 is not None:
            kw["op1"] = op1
        if accum_out is not None:
            kw["accum_out"] = accum_out
            wr.append(accum_out)
        return self.op(eng, lambda: h.tensor_scalar(out=out, in0=in0, scalar1=s1, scalar2=s2, op0=op0, **kw),
                       rd, wr)

    def stt(self, out, in0, scalar, in1, op0, op1):
        nc = self.nc
        rd = [in0, in1]
        if not isinstance(scalar, (int, float)):
            rd.append(scalar)
        return self.op("dve", lambda: nc.vector.scalar_tensor_tensor(out=out, in0=in0, scalar=scalar, in1=in1,
                                                                     op0=op0, op1=op1), rd, [out])

    def copy(self, out, in_, eng="dve"):
        nc = self.nc
        if eng == "act":
            return self.op("act", lambda: nc.scalar.copy(out=out, in_=in_), [in_], [out])
        h = self.h[eng]
        return self.op(eng, lambda: h.tensor_copy(out=out, in_=in_), [in_], [out])

    def memset(self, ap, val, eng="dve"):
        h = self.h[eng]
        return self.op(eng, lambda: h.memset(ap, val), [], [ap])

    def recip(self, out, in_):
        nc = self.nc
        return self.op("dve", lambda: nc.vector.reciprocal(out=out, in_=in_), [in_], [out])

    def dma(self, out, in_, q="sp", slow=False):
        h = self.h[q]
        if slow:
            return self.op(q, lambda: h.dma_start(out=out, in_=in_, allow_slow_non_contiguous=True),
                           [in_], [out], dma=True)
        return self.op(q, lambda: h.dma_start(out=out, in_=in_), [in_], [out], dma=True)

    def build(self):
        nc = self.nc
        sems = {}

        def newsem(nm):
            c = nc.semaphore(nm)
            s = c.__enter__()
            self.ctx.append(c)
            return s

        SEMCAP = 12000
        for e in self.ENG:
            sems[e] = []
        dsem = {q: [newsem("d_%s%d" % (q, i)) for i in range(DMA_ROT)] for q in ("sp", "act", "pool")}
        cnt = {e: 0 for e in self.ENG}
        dcnt = {q: 0 for q in dsem}
        for o in self.ops:
            o.prewait = None
            if o.dma:
                m = dcnt[o.eng]
                dcnt[o.eng] += 1
                o.sem = dsem[o.eng][m % DMA_ROT]
                o.val = 16 * (m // DMA_ROT + 1)
                if m // DMA_ROT > 0:
                    o.prewait = (o.sem, 16 * (m // DMA_ROT))
            elif o.marked:
                c_ = cnt[o.eng]
                cnt[o.eng] += 1
                if c_ // SEMCAP >= len(sems[o.eng]):
                    sems[o.eng].append(newsem("c_%s%d" % (o.eng, c_ // SEMCAP)))
                o.sem = sems[o.eng][c_ // SEMCAP]
                o.val = c_ % SEMCAP + 1
        waited = {e: {} for e in self.ENG}
        nwait = 0
        final_waits = {}
        for o in self.ops:
            h = self.h[o.eng]
            w = waited[o.eng]
            need = {}
            if o.prewait is not None:
                need[id(o.prewait[0])] = (o.prewait[0], o.prewait[1])
            for d in o.deps:
                p = self.ops[d]
                k = id(p.sem)
                if k not in need or need[k][1] < p.val:
                    need[k] = (p.sem, p.val)
            for k, (s, v) in need.items():
                if w.get(k, 0) >= v:
                    continue
                h.wait_ge(s, v)
                w[k] = v
                nwait += 1
            inst = o.fn()
            if o.dma:
                inst.then_inc(o.sem, 16)
                if any(r[0] in self.out_names for r in o.wr):
                    final_waits[id(o.sem)] = (o.sem, max(o.val, final_waits.get(id(o.sem), (None, 0))[1]))
            elif o.marked:
                inst.then_inc(o.sem, 1)
        for k, (s, v) in final_waits.items():
            nc.sync.wait_ge(s, v)
        self.stats = dict(nops=len(self.ops), nwait=nwait, cnt=cnt, dcnt=dcnt)
        return nc


def run_prog(prog, in_maps, n=8, trace=False):
    nc = prog.build()
    res = run_bass_kernel_spmd(nc, in_maps, core_ids=list(range(n)), trace=trace)
    return res


D = 2048
DFF = 5632
NFC = DFF // 128
KC = D // 128
EPS = 1e-6


def bcast_row(P, dst, row_ap, q="sp"):
    P.dma(dst, row_ap.partition_broadcast(128), q=q)


def transpose_rows(P, src, np_, tps, dstT, ident, col0=0):
    for half in range(2):
        tp = tps[half]
        for k8 in range(8):
            kc = half * 8 + k8
            P.tr(tp[:, k8, 0:np_], src[0:np_, kc * 128:(kc + 1) * 128], ident[0:np_, 0:np_])
        if half == 0:
            P.copy(dstT[:, 0:8, col0:col0 + np_], tp[:, :, 0:np_], eng="act")
        else:
            P.copy(dstT[:, 8:16, col0:col0 + np_], tp[:, :, 0:np_], eng="dve")


def rms_rstd(P, x, np_, junk, ssq, rstd):
    P.actv(junk[0:np_, :], x[0:np_, :], AF.Square, accum_out=ssq[0:np_, :])
    P.ts(ssq[0:np_, :], ssq[0:np_, :], 1.0 / D, ALU.mult, EPS, ALU.add)
    P.actv(ssq[0:np_, :], ssq[0:np_, :], AF.Sqrt)
    P.recip(rstd[0:np_, :], ssq[0:np_, :])


def build_ffn(segs, final, TB=512):
    P = Prog()
    nseg = len(segs)
    x_d = [P.dram("x%d" % i, [n + 2, D]) for i, n in enumerate(segs)]
    z_d = [P.dram("z%d" % i, [n + 2, D]) for i, n in enumerate(segs)]
    mod_d = [P.dram("mod%d" % i, [6, D]) for i in range(nseg)]
    hm_d = [P.dram("hmd%d" % i, [128, 2]) for i in range(nseg)]
    y_d = [P.dram("y%d" % i, [n, D], kind="ExternalOutput") for i, n in enumerate(segs)]
    wo_d = P.dram("wo_d", [D, D])
    g2_d = P.dram("g2", [1, D])
    fg_d = P.dram("fg", [1, D])
    wup_d = P.dram("wup", [D, 2 * DFF])
    cw_d = P.dram("cw_d", [128, NFC * 3])
    cb_d = P.dram("cb_d", [128, NFC])
    wdn_d = P.dram("wdn", [DFF, D])
    id_d = P.dram("ident_d", [128, 128])
    xm_d = [P.dram("xm%d" % i, [n + 2, D], kind="ExternalOutput") for i, n in enumerate(segs)]
    hT_d = [P.dram("hT%d" % i, [KC, 128, n + 2], BF16, kind="Internal") for i, n in enumerate(segs)]

    ident = P.sb("ident", [128, 128], BF16)
    P.dma(ident[:], id_d[:, :], q="pool")
    cw = P.sb("cw", [128, NFC * 3])
    cb = P.sb("cb", [128, NFC])
    P.dma(cw[:], cw_d[:, :])
    P.dma(cb[:], cb_d[:, :])
    hm = [P.sb("hm%d" % i, [128, 2]) for i in range(nseg)]
    for i in range(nseg):
        P.dma(hm[i][:], hm_d[i][:, :])
    ssq = P.sb("ssq", [128, 1])
    rstd = P.sb("rstd", [128, 1])

    mark = len(P.ctx)
    pb = [P.ps("pb%d" % i, [128, 512]) for i in range(4)]
    tps = [P.ps("tp%d" % i, [128, 8, 128], BF16) for i in range(2)]
    wo = P.sb("wo", [128, KC, D], BF16)
    wo_v = wo_d.rearrange("(k p) n -> p k n", p=128)
    for c in range(4):
        P.dma(wo[:, :, c * 512:(c + 1) * 512], wo_v[:, :, c * 512:(c + 1) * 512], q="pool")
    GT1 = P.sb("GT1", [128, D])
    G2 = P.sb("G2", [128, D])
    S2 = P.sb("S2", [128, D])
    tmpb = P.sb("tmpb", [128, D])
    zt = [P.sb("zt%d" % i, [128, D], BF16) for i in range(2)]
    xt = [P.sb("xt%d" % i, [128, D]) for i in range(2)]
    zT = [P.sb("zT%d" % i, [128, KC, 128], BF16) for i in range(2)]
    hf = [P.sb("hf%d" % i, [128, D], BF16) for i in range(2)]
    hfT = [P.sb("hfT%d" % i, [128, KC, 128], BF16) for i in range(2)]
    tmp5 = [P.sb("tmp5%d" % i, [128, 512]) for i in range(2)]
    it = 0
    for si, n in enumerate(segs):
        bcast_row(P, GT1[:], mod_d[si][2:3, :])
        bcast_row(P, S2[:], mod_d[si][3:4, :])
        bcast_row(P, tmpb[:], mod_d[si][4:5, :])
        bcast_row(P, G2[:], g2_d[0:1, :])
        P.ts(tmpb[:], tmpb[:], 1.0, ALU.add)
        P.tt(G2[:], G2[:], tmpb[:], ALU.mult)
        tiles = [(None, 2)] + [(1 + j * 128, 128) for j in range(n // 128)]
        for (r0, np_) in tiles:
            b = it % 2
            it += 1
            if r0 is None:
                P.dma(zt[b][0:1, :], z_d[si][0:1, :], q="pool")
                P.dma(zt[b][1:2, :], z_d[si][n + 1:n + 2, :], q="pool")
                P.dma(xt[b][0:1, :], x_d[si][0:1, :])
                P.dma(xt[b][1:2, :], x_d[si][n + 1:n + 2, :])
            else:
                P.dma(zt[b][:, :], z_d[si][r0:r0 + 128, :], q="pool")
                P.dma(xt[b][:, :], x_d[si][r0:r0 + 128, :])
            transpose_rows(P, zt[b], np_, tps, zT[b], ident)
            for c in range(4):
                po = pb[c]
                for kc in range(KC):
                    P.mm(po[0:np_, :], zT[b][:, kc, 0:np_], wo[:, kc, c * 512:(c + 1) * 512],
                         start=(kc == 0), stop=(kc == KC - 1))
                t5 = tmp5[c % 2]
                P.tt(t5[0:np_, :], po[0:np_, :], GT1[0:np_, c * 512:(c + 1) * 512], ALU.mult)
                P.tt(xt[b][0:np_, c * 512:(c + 1) * 512], t5[0:np_, :], xt[b][0:np_, c * 512:(c + 1) * 512],
                     ALU.add, eng="pool")
            if r0 is None:
                P.dma(xm_d[si][0:1, :], xt[b][0:1, :])
                P.dma(xm_d[si][n + 1:n + 2, :], xt[b][1:2, :])
            else:
                P.dma(xm_d[si][r0:r0 + 128, :], xt[b][:, :])
            rms_rstd(P, xt[b], np_, tmpb, ssq, rstd)
            P.stt(tmpb[0:np_, :], xt[b][0:np_, :], rstd[0:np_, :], G2[0:np_, :], ALU.mult, ALU.mult)
            P.tt(hf[b][0:np_, :], tmpb[0:np_, :], S2[0:np_, :], ALU.add)
            transpose_rows(P, hf[b], np_, tps, hfT[b], ident)
            hv = hT_d[si].rearrange("k p t -> p k t")
            if r0 is None:
                P.dma(hv[:, :, 0:1], hfT[b][:, :, 0:1], slow=True)
                P.dma(hv[:, :, n + 1:n + 2], hfT[b][:, :, 1:2], slow=True)
            else:
                P.dma(hv[:, :, r0:r0 + 128], hfT[b][:, :, 0:128])
    P.barrier()
    P.release(mark)

    pb = [P.ps("pq%d" % i, [128, 512]) for i in range(8)]
    GT2 = P.sb("GT2", [128, D])
    FG = P.sb("FG", [128, D])
    if final:
        bcast_row(P, FG[:], fg_d[0:1, :])
    uT = P.sb("uT", [128, NFC, TB], BF16)
    hb = P.sb("hb", [128, KC, TB + 2], BF16)
    wg = [P.sb("wg%d" % i, [128, KC, 256], BF16) for i in range(2)]
    wv = [P.sb("wv%d" % i, [128, KC, 256], BF16) for i in range(2)]
    wd = [P.sb("wd%d" % i, [128, 11, 512], BF16) for i in range(2)]
    yt = [P.sb("yt%d" % i, [128, D]) for i in range(4)]
    gs = [P.sb("gs%d" % i, [128, TB + 2]) for i in range(2)]
    t1 = [P.sb("t1%d" % i, [128, TB]) for i in range(2)]
    sg = [P.sb("sg%d" % i, [128, TB]) for i in range(2)]
    tmp5 = [P.sb("tmq%d" % i, [128, 512]) for i in range(2)]
    junk = P.sb("junk", [128, D])
    wup_v = wup_d.rearrange("(k p) n -> p k n", p=128)
    wdn_v = wdn_d.rearrange("(f p) n -> p f n", p=128)
    pg = [pb[0], pb[1]]
    pv = pb[2]
    ph = pb[3]
    pd = [pb[4], pb[5], pb[6], pb[7]]
    gi = 0
    di = 0
    for si, n in enumerate(segs):
        bcast_row(P, GT2[:], mod_d[si][5:6, :])
        hv = hT_d[si].rearrange("k p t -> p k t")
        nblk = (n + TB - 1) // TB
        for bi in range(nblk):
            j0 = bi * TB
            tb = min(TB, n - j0)
            P.dma(hb[:, :, 0:tb + 2], hv[:, :, j0:j0 + tb + 2])
            for grp in range(NFC // 2):
                wb = gi % 2
                gi += 1
                P.dma(wg[wb][:], wup_v[:, :, grp * 256:(grp + 1) * 256], q="pool")
                P.dma(wv[wb][:], wup_v[:, :, DFF + grp * 256:DFF + (grp + 1) * 256], q="pool")
                for jj in range(2):
                    fc = grp * 2 + jj
                    g = pg[fc % 2]
                    gsb = gs[fc % 2]
                    for kc in range(KC):
                        P.mm(g[:, 0:tb], wg[wb][:, kc, jj * 128:(jj + 1) * 128], hb[:, kc, 1:tb + 1],
                             start=(kc == 0), stop=(kc == KC - 1))
                    for kc in range(KC):
                        P.mm(ph[:, 0:2], wg[wb][:, kc, jj * 128:(jj + 1) * 128], hb[:, kc, 0:tb + 2:tb + 1],
                             start=(kc == 0), stop=(kc == KC - 1))
                    for kc in range(KC):
                        P.mm(pv[:, 0:tb], wv[wb][:, kc, jj * 128:(jj + 1) * 128], hb[:, kc, 1:tb + 1],
                             start=(kc == 0), stop=(kc == KC - 1))
                    P.copy(gsb[:, 1:tb + 1], g[:, 0:tb], eng="act")
                    if bi == 0:
                        P.ts(gsb[:, 0:1], ph[:, 0:1], hm[si][:, 0:1], ALU.mult)
                    else:
                        P.copy(gsb[:, 0:1], ph[:, 0:1])
                    if bi == nblk - 1:
                        P.ts(gsb[:, tb + 1:tb + 2], ph[:, 1:2], hm[si][:, 1:2], ALU.mult)
                    else:
                        P.copy(gsb[:, tb + 1:tb + 2], ph[:, 1:2])
                    tt1 = t1[fc % 2]
                    P.ts(tt1[:, 0:tb], gsb[:, 0:tb], cw[:, fc * 3:fc * 3 + 1], ALU.mult, cb[:, fc:fc + 1], ALU.add)
                    P.stt(tt1[:, 0:tb], gsb[:, 1:tb + 1], cw[:, fc * 3 + 1:fc * 3 + 2], tt1[:, 0:tb],
                          ALU.mult, ALU.add)
                    P.stt(tt1[:, 0:tb], gsb[:, 2:tb + 2], cw[:, fc * 3 + 2:fc * 3 + 3], tt1[:, 0:tb],
                          ALU.mult, ALU.add)
                    sgb = sg[fc % 2]
                    P.actv(sgb[:, 0:tb], tt1[:, 0:tb], AF.Silu)
                    P.tt(uT[:, fc, 0:tb], sgb[:, 0:tb], pv[:, 0:tb], ALU.mult)
            ntile = (tb + 127) // 128
            for ti in range(ntile):
                r = j0 + ti * 128
                P.dma(yt[ti][:, :], xm_d[si][1 + r:1 + r + 128, :])
            for c in range(4):
                for q in range(4):
                    db = di % 2
                    di += 1
                    P.dma(wd[db][:], wdn_v[:, q * 11:(q + 1) * 11, c * 512:(c + 1) * 512], q="pool")
                    for ti in range(ntile):
                        for f in range(11):
                            P.mm(pd[ti][:, :], uT[:, q * 11 + f, ti * 128:(ti + 1) * 128], wd[db][:, f, :],
                                 start=(q == 0 and f == 0), stop=(q == 3 and f == 10))
                for ti in range(ntile):
                    t5 = tmp5[ti % 2]
                    P.tt(t5[:], pd[ti][:, :], GT2[:, c * 512:(c + 1) * 512], ALU.mult)
                    P.tt(yt[ti][:, c * 512:(c + 1) * 512], t5[:], yt[ti][:, c * 512:(c + 1) * 512], ALU.add,
                         eng="pool")
            for ti in range(ntile):
                r = j0 + ti * 128
                if final:
                    rms_rstd(P, yt[ti], 128, junk, ssq, rstd)
                    P.stt(yt[ti][:, :], yt[ti][:, :], rstd[:, :], FG[:, :], ALU.mult, ALU.mult)
                P.dma(y_d[si][r:r + 128, :], yt[ti][:, :])
    return P


ADA_W = 1536


def build_ada():
    P = Prog()
    cT_d = P.dram("cT_d", [128, KC * 8])
    aw_d = P.dram("aw_d", [2, D, ADA_W])
    ab_d = P.dram("ab_d", [2, ADA_W])
    m_d = P.dram("m", [2, 8, ADA_W], kind="ExternalOutput")
    cT = P.sb("cT", [128, KC * 8])
    sT = P.sb("sT", [128, KC, 8], BF16)
    P.dma(cT[:], cT_d[:, :])
    P.actv(sT[:, :, :].rearrange("p k r -> p (k r)"), cT[:], AF.Silu)
    awt = [P.sb("awt%d" % i, [128, KC, 512], BF16) for i in range(2)]
    abt = [P.sb("abt%d" % i, [8, 512]) for i in range(2)]
    mt = [P.sb("mt%d" % i, [8, 512]) for i in range(2)]
    ps = [P.ps("ps%d" % i, [128, 512]) for i in range(2)]
    it = 0
    for i in range(2):
        av = aw_d[i].rearrange("(k p) n -> p k n", p=128)
        for c in range(ADA_W // 512):
            b = it % 2
            it += 1
            P.dma(awt[b][:], av[:, :, c * 512:(c + 1) * 512], q="pool")
            P.dma(abt[b][:], ab_d[i:i + 1, c * 512:(c + 1) * 512].partition_broadcast(8))
            for kc in range(KC):
                P.mm(ps[b][0:8, :], sT[:, kc, :], awt[b][:, kc, :], start=(kc == 0), stop=(kc == KC - 1))
            P.tt(mt[b][:], ps[b][0:8, :], abt[b][:], ALU.add)
            P.dma(m_d[i, :, c * 512:(c + 1) * 512], mt[b][:])
    return P


NH = 16
QL = 512
KVL = 512
NCTX = 256
NLAT = 4096
NK = NCTX + NLAT
NQ = 2048
SCALE = 192.0 ** -0.5


def build_mla(HG=2):
    P = Prog()
    xk_d = P.dram("xk", [NK, D])
    xq_d = P.dram("xq", [NQ, D])
    modl_d = P.dram("modl", [6, D])
    modc_d = P.dram("modc", [6, D])
    g1_d = P.dram("g1", [1, D])
    wd_d = P.dram("wdown", [D, 1088])
    qn_d = P.dram("qn", [1, QL])
    kvn_d = P.dram("kvn", [1, KVL])
    wuq_d = P.dram("wuq_d", [QL, NH * 192])
    wukv_d = P.dram("wukv_d", [KVL, NH * 256])
    ctk_d = P.dram("ctk_d", [64, NK])
    stk_d = P.dram("stk_d", [64, NK])
    ctq_d = P.dram("ctq_d", [64, NQ])
    stq_d = P.dram("stq_d", [64, NQ])
    id_d = P.dram("ident_d", [128, 128])
    o_d = P.dram("o", [NQ, NH * 128], kind="ExternalOutput")

    ident = P.sb("ident", [128, 128], BF16)
    P.dma(ident[:], id_d[:, :], q="pool")
    ssq = P.sb("ssq", [128, 1])
    rstd = P.sb("rstd", [128, 1])
    ckvT = P.sb("ckvT", [128, 4, NK], BF16)
    cqT = P.sb("cqT", [128, 4, NQ], BF16)
    krT = P.sb("krT", [64, NK], BF16)

    mark = len(P.ctx)
    pb = [P.ps("pa%d" % i, [128, 512]) for i in range(4)]
    tps = [P.ps("tp%d" % i, [128, 8, 128], BF16) for i in range(2)]
    wd = P.sb("wd", [128, KC, 1088], BF16)
    wd_v = wd_d.rearrange("(k p) n -> p k n", p=128)
    P.dma(wd[:, :, 0:512], wd_v[:, :, 0:512], q="pool")
    P.dma(wd[:, :, 512:1088], wd_v[:, :, 512:1088], q="pool")
    wds = P.sb("wds", [128, KC, 64], BF16)
    for a in range(2):
        P.dma(wds[:, :, a * 32:a * 32 + 16], wd_v[:, :, 1024 + a * 32 + 16:1024 + a * 32 + 32], q="pool")
        P.dma(wds[:, :, a * 32 + 16:a * 32 + 32], wd_v[:, :, 1024 + a * 32:1024 + a * 32 + 16], q="pool")
    ctk = [P.sb("ctk%d" % i, [64, 128]) for i in range(2)]
    stk = [P.sb("stk%d" % i, [64, 128]) for i in range(2)]
    G1 = {}
    S1 = {}
    tmpb = P.sb("tmpb", [128, D])
    for nm, md in (("l", modl_d), ("c", modc_d)):
        G1[nm] = P.sb("G1" + nm, [128, D])
        S1[nm] = P.sb("S1" + nm, [128, D])
        bcast_row(P, S1[nm][:], md[0:1, :])
        bcast_row(P, tmpb[:], md[1:2, :])
        bcast_row(P, G1[nm][:], g1_d[0:1, :])
        P.ts(tmpb[:], tmpb[:], 1.0, ALU.add)
        P.tt(G1[nm][:], G1[nm][:], tmpb[:], ALU.mult)
    QN = P.sb("QN", [128, QL])
    KVN = P.sb("KVN", [128, KVL])
    bcast_row(P, QN[:], qn_d[0:1, :])
    bcast_row(P, KVN[:], kvn_d[0:1, :])
    xt = [P.sb("xt%d" % i, [128, D]) for i in range(2)]
    hb = [P.sb("hb%d" % i, [128, D], BF16) for i in range(2)]
    hT = [P.sb("hT%d" % i, [128, KC, 128], BF16) for i in range(2)]
    cn = [P.sb("cn%d" % i, [128, 512], BF16) for i in range(2)]
    junk5 = P.sb("junk5", [128, 512])
    ra = P.sb("ra", [64, 128])
    rb = P.sb("rb", [64, 128])
    it = 0
    tiles = []
    for j in range(NK // 128):
        tiles.append((xk_d, j * 128, "k", "c" if j < NCTX // 128 else "l", j * 128))
    for j in range(NQ // 128):
        tiles.append((xq_d, j * 128, "q", "l", j * 128))
    for (src, r0, kind, mn, c0) in tiles:
        b = it % 2
        it += 1
        P.dma(xt[b][:, :], src[r0:r0 + 128, :])
        rms_rstd(P, xt[b], 128, tmpb, ssq, rstd)
        P.stt(tmpb[:, :], xt[b][:, :], rstd[:, :], G1[mn][:, :], ALU.mult, ALU.mult)
        P.tt(hb[b][:, :], tmpb[:, :], S1[mn][:, :], ALU.add)
        transpose_rows(P, hb[b], 128, tps, hT[b], ident)
        if kind == "k":
            cp = pb[0]
            for kc in range(KC):
                P.mm(cp[:, :], hT[b][:, kc, :], wd[:, kc, 512:1024], start=(kc == 0), stop=(kc == KC - 1))
            nrm = KVN
            dstT = ckvT
            pr = pb[2]
            for kc in range(KC):
                P.mm(pr[0:64, 0:128], wd[:, kc, 1024:1088], hT[b][:, kc, :], start=(kc == 0), stop=(kc == KC - 1))
            pr2 = pb[3]
            for kc in range(KC):
                P.mm(pr2[0:64, 0:128], wds[:, kc, :], hT[b][:, kc, :], start=(kc == 0), stop=(kc == KC - 1))
            P.dma(ctk[b][:, :], ctk_d[:, c0:c0 + 128])
            P.dma(stk[b][:, :], stk_d[:, c0:c0 + 128])
            P.tt(ra[:, :], pr[0:64, 0:128], ctk[b][:, :], ALU.mult)
            P.tt(rb[:, :], pr2[0:64, 0:128], stk[b][:, :], ALU.mult)
            P.tt(krT[:, c0:c0 + 128], ra[:, :], rb[:, :], ALU.add)
        else:
            cp = pb[1]
            for kc in range(KC):
                P.mm(cp[:, :], hT[b][:, kc, :], wd[:, kc, 0:512], start=(kc == 0), stop=(kc == KC - 1))
            nrm = QN
            dstT = cqT
        P.actv(junk5[:, :], cp[:, :], AF.Square, accum_out=ssq[:, :])
        P.ts(ssq[:, :], ssq[:, :], 1.0 / 512, ALU.mult, EPS, ALU.add)
        P.actv(ssq[:, :], ssq[:, :], AF.Sqrt)
        P.recip(rstd[:, :], ssq[:, :])
        P.stt(cn[b][:, :], cp[:, :], rstd[:, :], nrm[:, :], ALU.mult, ALU.mult)
        tp = tps[0]
        for k4 in range(4):
            P.tr(tp[:, k4, :], cn[b][:, k4 * 128:(k4 + 1) * 128], ident[:, :])
        P.copy(dstT[:, :, c0:c0 + 128], tp[:, 0:4, :], eng="act")
    P.barrier()
    P.release(mark)

    ctq = P.sb("ctq", [64, NQ])
    stq = P.sb("stq", [64, NQ])
    P.dma(ctq[:], ctq_d[:, :])
    P.dma(stq[:], stq_d[:, :])
    ps_s = [P.ps("ps_s%d" % i, [128, 512]) for i in range(2)]
    ps_o = [P.ps("ps_o%d" % i, [128, 512]) for i in range(4)]
    ps_p = [P.ps("ps_p%d" % i, [128, 512]) for i in range(2)]
    wukv = P.sb("wukv", [128, 4, HG * 256], BF16)
    wuq = P.sb("wuq", [128, 4, HG * 192], BF16)
    wuqs = P.sb("wuqs", [128, 4, HG * 64], BF16)
    kT = P.sb("kT", [128, HG, NK], BF16)
    NKT = NK // 128
    Va = P.sb("Va", [128, NKT, HG, 129], BF16)
    P.memset(Va[:, :, :, 128:129], 1.0)
    qT = P.sb("qT", [128, HG, NQ], BF16)
    qrT = P.sb("qrT", [64, HG, NQ], BF16)
    pT = [P.sb("pT%d" % i, [128, 512], BF16) for i in range(3)]
    osb = [P.sb("osb%d" % i, [128, HG * 128]) for i in range(4)]
    rs = P.sb("rs", [128, 1])
    ra = P.sb("ra2", [64, 512])
    rb = P.sb("rb2", [64, 512])
    wukv_v = wukv_d.rearrange("(k p) n -> p k n", p=128)
    wuq_v = wuq_d.rearrange("(k p) n -> p k n", p=128)
    pi = 0
    for hg in range(NH // HG):
        h0 = hg * HG
        P.dma(wukv[:], wukv_v[:, :, h0 * 256:(h0 + HG) * 256], q="pool")
        P.dma(wuq[:], wuq_v[:, :, h0 * 192:(h0 + HG) * 192], q="pool")
        for hh in range(HG):
            base = (h0 + hh) * 192 + 128
            for a in range(2):
                P.dma(wuqs[:, :, hh * 64 + a * 32:hh * 64 + a * 32 + 16],
                      wuq_v[:, :, base + a * 32 + 16:base + a * 32 + 32], q="pool")
                P.dma(wuqs[:, :, hh * 64 + a * 32 + 16:hh * 64 + a * 32 + 32],
                      wuq_v[:, :, base + a * 32:base + a * 32 + 16], q="pool")
        for hh in range(HG):
            for tb0 in range(0, NK, 512):
                tb = min(512, NK - tb0)
                pp = ps_p[pi % 2]
                pi += 1
                for kc in range(4):
                    P.mm(pp[:, 0:tb], wukv[:, kc, hh * 256:hh * 256 + 128], ckvT[:, kc, tb0:tb0 + tb],
                         start=(kc == 0), stop=(kc == 3))
                P.copy(kT[:, hh, tb0:tb0 + tb], pp[:, 0:tb], eng=("act" if pi % 2 else "dve"))
            for kt in range(NKT):
                pp = ps_p[pi % 2]
                pi += 1
                for kc in range(4):
                    P.mm(pp[:, 0:128], ckvT[:, kc, kt * 128:(kt + 1) * 128],
                         wukv[:, kc, hh * 256 + 128:hh * 256 + 256], start=(kc == 0), stop=(kc == 3))
                P.copy(Va[:, kt, hh, 0:128], pp[:, 0:128], eng=("act" if pi % 2 else "dve"))
            for qb in range(NQ // 512):
                pp = ps_p[pi % 2]
                pi += 1
                for kc in range(4):
                    P.mm(pp[:, :], wuq[:, kc, hh * 192:hh * 192 + 128], cqT[:, kc, qb * 512:(qb + 1) * 512],
                         start=(kc == 0), stop=(kc == 3))
                P.copy(qT[:, hh, qb * 512:(qb + 1) * 512], pp[:, :], eng=("act" if pi % 2 else "dve"))
                pp = ps_p[pi % 2]
                pi += 1
                for kc in range(4):
                    P.mm(pp[0:64, :], wuq[:, kc, hh * 192 + 128:hh * 192 + 192], cqT[:, kc, qb * 512:(qb + 1) * 512],
                         start=(kc == 0), stop=(kc == 3))
                P.tt(ra[:, :], pp[0:64, :], ctq[:, qb * 512:(qb + 1) * 512], ALU.mult)
                pp = ps_p[pi % 2]
                pi += 1
                for kc in range(4):
                    P.mm(pp[0:64, :], wuqs[:, kc, hh * 64:(hh + 1) * 64], cqT[:, kc, qb * 512:(qb + 1) * 512],
                         start=(kc == 0), stop=(kc == 3))
                P.tt(rb[:, :], pp[0:64, :], stq[:, qb * 512:(qb + 1) * 512], ALU.mult)
                P.tt(qrT[:, hh, qb * 512:(qb + 1) * 512], ra[:, :], rb[:, :], ALU.add)
        si = 0
        for qb in range(NQ // 512):
            for hh in range(HG):
                for kt in range(NKT):
                    sp_ = ps_s[si % 2]
                    pt = pT[si % 3]
                    si += 1
                    P.mm(sp_[:, :], kT[:, hh, kt * 128:(kt + 1) * 128], qT[:, hh, qb * 512:(qb + 1) * 512],
                         start=True, stop=False)
                    P.mm(sp_[:, :], krT[:, kt * 128:(kt + 1) * 128], qrT[:, hh, qb * 512:(qb + 1) * 512],
                         start=False, stop=True)
                    P.actv(pt[:, :], sp_[:, :], AF.Exp, scale=SCALE)
                    for qi in range(4):
                        P.mm(ps_o[qi][:, 0:129], pt[:, qi * 128:(qi + 1) * 128], Va[:, kt, hh, :],
                             start=(kt == 0), stop=(kt == NKT - 1))
                for qi in range(4):
                    P.recip(rs[:, :], ps_o[qi][:, 128:129])
                    P.ts(osb[qi][:, hh * 128:(hh + 1) * 128], ps_o[qi][:, 0:128], rs[:, :], ALU.mult)
            for qi in range(4):
                r = qb * 512 + qi * 128
                P.dma(o_d[r:r + 128, h0 * 128:(h0 + HG) * 128], osb[qi][:, :])
    return P


NEG_EHALF = -0.6065306597126334


class _Stop(Exception):
    pass


def build_rwkva(segs, TB=512, STAGE=99, P=None):
    P = P or Prog()

    def ck(k):
        if STAGE == k:
            raise _Stop()
    nseg = len(segs)
    x_d = [P.dram("x%d" % i, [n + 2, D]) for i, n in enumerate(segs)]
    modT_d = [P.dram("modT%d" % i, [128, 6 * KC]) for i in range(nseg)]
    hm_d = [P.dram("hmd%d" % i, [128, 2]) for i in range(nseg)]
    gT_d = P.dram("gT_d", [128, KC])
    mu_d = P.dram("mu_d", [128, 6 * KC])
    wr_d = P.dram("wr_d", [D, D])
    wk_d = P.dram("wk_d", [D, D])
    wv_d = P.dram("wv_d", [D, D])
    w1_d = P.dram("w1_d", [2, D, 96])
    w2_d = P.dram("w2_d", [2, 96, D])
    w0_d = P.dram("w0_d", [2, D])
    a1_d = P.dram("a1_d", [2, D, 96])
    a2_d = P.dram("a2_d", [2, 96, D])
    a0_d = P.dram("a0_d", [2, D])
    g1_d = P.dram("g1_d", [D, 256])
    g2_d = P.dram("g2_d", [256, D])
    id_d = P.dram("ident_d", [128, 128])
    o_r = [P.dram("or%d" % i, [n, D], kind="ExternalOutput") for i, n in enumerate(segs)]
    o_k = [P.dram("ok%d" % i, [n, D], kind="ExternalOutput") for i, n in enumerate(segs)]
    o_v = [P.dram("ov%d" % i, [n, D], kind="ExternalOutput") for i, n in enumerate(segs)]
    o_g = [P.dram("og%d" % i, [n, D], kind="ExternalOutput") for i, n in enumerate(segs)]
    o_lw = [[P.dram("olw%d_%d" % (d, i), [n, D], kind="ExternalOutput") for i, n in enumerate(segs)] for d in range(2)]
    o_ic = [[P.dram("oic%d_%d" % (d, i), [n, D], kind="ExternalOutput") for i, n in enumerate(segs)] for d in range(2)]

    ident = P.sb("ident", [128, 128], BF16)
    P.dma(ident[:], id_d[:, :], q="pool")
    mu = P.sb("mu", [128, 6 * KC])
    P.dma(mu[:], mu_d[:, :])
    gT = P.sb("gT", [128, KC])
    P.dma(gT[:], gT_d[:, :])
    modT = P.sb("modT", [128, 6 * KC])
    G1c = P.sb("G1c", [128, KC])
    hm = P.sb("hm", [128, 2])
    ssq = P.sb("ssq", [128, 1])
    rstd = P.sb("rstd", [128, 1])
    w1 = P.sb("w1", [128, KC, 256], BF16)
    a1 = P.sb("a1", [128, KC, 256], BF16)
    P.memset(w1[:], 0.0)
    P.memset(a1[:], 0.0, eng="pool")
    for d in range(2):
        P.dma(w1[:, :, d * 128:d * 128 + 96], w1_d[d].rearrange("(k p) n -> p k n", p=128), q="pool")
        P.dma(a1[:, :, d * 128:d * 128 + 96], a1_d[d].rearrange("(k p) n -> p k n", p=128), q="pool")
    g1 = P.sb("g1", [128, KC, 256], BF16)
    P.dma(g1[:], g1_d.rearrange("(k p) n -> p k n", p=128), q="pool")
    w2 = P.sb("w2", [128, 2, D], BF16)
    a2 = P.sb("a2", [128, 2, D], BF16)
    P.memset(w2[:], 0.0)
    P.memset(a2[:], 0.0, eng="pool")
    for d in range(2):
        P.dma(w2[0:96, d, :], w2_d[d, :, :], q="pool")
        P.dma(a2[0:96, d, :], a2_d[d, :, :], q="pool")
    g2 = P.sb("g2", [128, 2, D], BF16)
    P.dma(g2[:], g2_d.rearrange("(m p) n -> p m n", p=128), q="pool")
    W0b = [P.sb("W0b%d" % d, [128, D]) for d in range(2)]
    A0b = [P.sb("A0b%d" % d, [128, D]) for d in range(2)]
    for d in range(2):
        P.dma(W0b[d][:], w0_d[d:d + 1, :].partition_broadcast(128))
        P.dma(A0b[d][:], a0_d[d:d + 1, :].partition_broadcast(128))
    xt = P.sb("xt", [128, D])
    xn = P.sb("xn", [128, D], BF16)
    tmpT = P.sb("tmpT", [128, KC, 128], BF16)
    hT = P.sb("hT", [128, KC, TB + 2], BF16)
    xx = P.sb("xx", [128, KC, TB], BF16)
    xj = P.sb("xj", [128, KC, TB], BF16)
    Wb = [P.sb("Wb%d" % i, [128, KC, 512], BF16) for i in range(2)]
    mid = P.sb("mid", [128, 2, TB], BF16)
    ot = [P.sb("ot%d" % i, [128, 512]) for i in range(2)]
    otb = [P.sb("otb%d" % i, [128, 512]) for i in range(2)]
    tps = [P.ps("tp%d" % i, [128, 8, 128], BF16) for i in range(2)]
    pacc = [P.ps("pacc%d" % i, [128, 512]) for i in range(4)]
    pmid = P.ps("pmid", [128, 512])
    pout = P.ps("pout", [128, 512])
    wi = 0
    oi = 0
    wsrc = [wr_d.rearrange("(k p) n -> p k n", p=128), None, wk_d.rearrange("(k p) n -> p k n", p=128),
            wv_d.rearrange("(k p) n -> p k n", p=128)]
    ck(1)
    for si, n in enumerate(segs):
        P.dma(modT[:], modT_d[si][:, :])
        P.dma(hm[:], hm_d[si][:, :])
        P.ts(G1c[:], modT[:, KC:2 * KC], 1.0, ALU.add)
        P.tt(G1c[:], G1c[:], gT[:], ALU.mult)
        nblk = (n + TB - 1) // TB
        for bi in range(nblk):
            j0 = bi * TB
            tb = min(TB, n - j0)
            tiles = [(None, 2, 0)] + [(1 + j0 + t * 128, 128, 1 + t * 128) for t in range(tb // 128)]
            for (r0, np_, c0) in tiles:
                if r0 is None:
                    P.dma(xt[0:1, :], x_d[si][j0:j0 + 1, :])
                    P.dma(xt[1:2, :], x_d[si][j0 + tb + 1:j0 + tb + 2, :])
                else:
                    P.dma(xt[:, :], x_d[si][r0:r0 + 128, :])
                rms_rstd(P, xt, np_, xn, ssq, rstd)
                P.ts(xn[0:np_, :], xt[0:np_, :], rstd[0:np_, :], ALU.mult)
                transpose_rows(P, xn, np_, tps, tmpT, ident)
                for kc in range(KC):
                    if r0 is None:
                        P.ts(hT[:, kc, 0:tb + 2:tb + 1], tmpT[:, kc, 0:2], G1c[:, kc:kc + 1], ALU.mult,
                             modT[:, kc:kc + 1], ALU.add)
                    else:
                        P.ts(hT[:, kc, c0:c0 + 128], tmpT[:, kc, 0:128], G1c[:, kc:kc + 1], ALU.mult,
                             modT[:, kc:kc + 1], ALU.add)
                if r0 is None:
                    if bi == 0:
                        P.ts(hT[:, :, 0:1], hT[:, :, 0:1], hm[:, 0:1], ALU.mult)
                    if bi == nblk - 1:
                        P.ts(hT[:, :, tb + 1:tb + 2], hT[:, :, tb + 1:tb + 2], hm[:, 1:2], ALU.mult)
            ck(2)
            P.tt(xx[:, :, 0:tb], hT[:, :, 0:tb], hT[:, :, 2:tb + 2], ALU.add)
            P.stt(xx[:, :, 0:tb], xx[:, :, 0:tb], 0.5, hT[:, :, 1:tb + 1], ALU.mult, ALU.subtract)
            ck(3)
            ntile = tb // 128
            rows = lambda t: slice(j0 + t * 128, j0 + (t + 1) * 128)
            for j in range(6):
                for kc in range(KC):
                    P.stt(xj[:, kc, 0:tb], xx[:, kc, 0:tb], mu[:, j * KC + kc:j * KC + kc + 1], hT[:, kc, 1:tb + 1],
                          ALU.mult, ALU.add)
                ck(4 + j * 2)
                if j in (0, 2, 3):
                    dst = {0: o_r, 2: o_k, 3: o_v}[j][si]
                    for cb in range(4):
                        wb = Wb[wi % 2]
                        wi += 1
                        P.dma(wb[:], wsrc[j][:, :, cb * 512:(cb + 1) * 512], q="pool")
                        for t in range(ntile):
                            for kc in range(KC):
                                P.mm(pacc[t][:, :], xj[:, kc, t * 128:(t + 1) * 128], wb[:, kc, :],
                                     start=(kc == 0), stop=(kc == KC - 1))
                        for t in range(ntile):
                            ob = otb[oi % 2]
                            oi += 1
                            P.copy(ob[:], pacc[t][:, :], eng=("act" if oi % 2 else "dve"))
                            P.dma(dst[rows(t), cb * 512:(cb + 1) * 512], ob[:])
                elif j in (1, 4):
                    lw1, lw2, bias, dsts = (w1, w2, W0b, o_lw) if j == 1 else (a1, a2, A0b, o_ic)
                    for d in range(2):
                        for kc in range(KC):
                            P.mm(pmid[:, 0:tb], lw1[:, kc, d * 128:(d + 1) * 128], xj[:, kc, 0:tb],
                                 start=(kc == 0), stop=(kc == KC - 1))
                        if j == 1:
                            P.actv(mid[:, 0, 0:tb], pmid[:, 0:tb], AF.Tanh)
                        else:
                            P.copy(mid[:, 0, 0:tb], pmid[:, 0:tb], eng="act")
                        for t in range(ntile):
                            for cb in range(4):
                                P.mm(pout[:, :], mid[:, 0, t * 128:(t + 1) * 128], lw2[:, d, cb * 512:(cb + 1) * 512])
                                o = ot[oi % 2]
                                oi += 1
                                P.tt(o[:], pout[:, :], bias[d][:, cb * 512:(cb + 1) * 512], ALU.add)
                                P.actv(o[:], o[:], AF.Sigmoid)
                                if j == 1:
                                    P.ts(o[:], o[:], NEG_EHALF, ALU.mult, eng="pool")
                                P.dma(dsts[d][si][rows(t), cb * 512:(cb + 1) * 512], o[:])
                else:
                    for m in range(2):
                        for kc in range(KC):
                            P.mm(pmid[:, 0:tb], g1[:, kc, m * 128:(m + 1) * 128], xj[:, kc, 0:tb],
                                 start=(kc == 0), stop=(kc == KC - 1))
                        P.actv(mid[:, m, 0:tb], pmid[:, 0:tb], AF.Sigmoid)
                    for t in range(ntile):
                        for cb in range(4):
                            for m in range(2):
                                P.mm(pout[:, :], mid[:, m, t * 128:(t + 1) * 128], g2[:, m, cb * 512:(cb + 1) * 512],
                                     start=(m == 0), stop=(m == 1))
                            ob = otb[oi % 2]
                            oi += 1
                            P.copy(ob[:], pout[:, :], eng=("act" if oi % 2 else "dve"))
                            P.dma(o_g[si][rows(t), cb * 512:(cb + 1) * 512], ob[:])
    return P


NT = 4352
NCH = NT // 128
CW = 1024
NPAIR = 8
NHD = 16
GN_EPS = 64e-5
ORDER = [list(range(NCH)), [1, 0] + list(range(NCH - 1, 1, -1))]


def build_rwkvb(ND=2, NCK=NCH, STAGE=99, NPL=NPAIR, DEBUG=False):
    P = Prog()
    r_d = P.dram("r_d", [NT, CW])
    k_d = P.dram("k_d", [NT, CW])
    v_d = P.dram("v_d", [NT, CW])
    g_d = P.dram("g_d", [NT, CW])
    lw_d = [P.dram("lwd%d" % d, [NT, CW]) for d in range(2)]
    ic_d = [P.dram("icd%d" % d, [NT, CW]) for d in range(2)]
    par_d = P.dram("par_d", [5, CW])
    tri_d = P.dram("tri_d", [2, 128, 128])
    mx_d = P.dram("maskx", [2, 128, 512])
    mz_d = P.dram("maskz", [2, 128, 256])
    id_d = P.dram("ident_d", [128, 128])
    z_d = P.dram("z", [NT, CW], kind="ExternalOutput")
    yf_d = P.dram("yf", [NT, CW], kind="ExternalOutput")
    bv_d = P.dram("bvs", [NT, CW], kind="Internal")
    yt_d = P.dram("ytot", [NT, CW], kind="ExternalOutput") if DEBUG else None

    ident = P.sb("ident", [128, 128], BF16)
    P.dma(ident[:], id_d[:, :], q="pool")
    ones = P.sb("ones", [128, 1])
    P.memset(ones[:], 1.0)
    PAR = [P.sb("par%d" % i, [128, CW]) for i in range(5)]
    for i in range(5):
        P.dma(PAR[i][:], par_d[i:i + 1, :].partition_broadcast(128))
    KK_, KA_, GNW, GNB, RK = PAR
    tri = P.sb("tri", [128, 128])
    mx = P.sb("mx", [128, 512], BF16)
    mz = P.sb("mz", [128, 256], BF16)
    rt = P.sb("rt", [128, CW], BF16)
    kt = P.sb("kt", [128, CW], BF16)
    vt = P.sb("vt", [128, CW], BF16)
    gt = P.sb("gt", [128, CW], BF16)
    lw = P.sb("lw", [128, CW])
    ic = P.sb("ic", [128, CW])
    f1 = P.sb("f1", [128, CW])
    f2 = P.sb("f2", [128, CW])
    f3 = P.sb("f3", [128, CW])
    kk = P.sb("kk", [128, CW])
    kd = P.sb("kd", [128, CW])
    Ep = P.sb("Ep", [128, CW])
    Em = P.sb("Em", [128, CW])
    Epx = P.sb("Epx", [128, CW])
    s16 = P.sb("s16", [128, NHD])
    s16b = P.sb("s16b", [128, NHD])
    RT_ = P.sb("RT_", [128, CW], BF16)
    AT_ = P.sb("AT_", [128, CW], BF16)
    BT_ = P.sb("BT_", [128, CW], BF16)
    KT_ = P.sb("KT_", [128, CW], BF16)
    ARf = P.sb("ARf", [128, NPAIR, 2, 128], BF16)
    BTf = P.sb("BTf", [128, NPAIR, 128], BF16)
    ATz = P.sb("ATz", [128, NPAIR, 2, 128], BF16)
    BTz = P.sb("BTz", [128, NPAIR, 2, 128], BF16)
    KTz = P.sb("KTz", [128, NPAIR, 2, 128], BF16)
    P.memset(ATz[:], 0.0)
    P.memset(BTz[:], 0.0, eng="pool")
    P.memset(KTz[:], 0.0)
    PC = P.sb("PC", [128, NPAIR])
    ysb = P.sb("ysb", [128, CW])
    yfl = P.sb("yfl", [128, CW])
    bvl = P.sb("bvl", [128, CW])
    H = P.sb("H", [128, NPAIR, 64])
    Hb = P.sb("Hb", [128, NPAIR, 2, 64], BF16)
    SX = [P.sb("SX%d" % i, [128, 512], BF16) for i in range(2)]
    SY = [P.sb("SY%d" % i, [128, 512], BF16) for i in range(2)]
    SZ = [P.sb("SZ%d" % i, [128, 256], BF16) for i in range(2)]
    XY = [P.sb("XY%d" % i, [128, 512]) for i in range(2)]
    Tt = [P.sb("Tt%d" % i, [128, 256]) for i in range(2)]
    Ttb = [P.sb("Ttb%d" % i, [128, 256], BF16) for i in range(2)]
    identf = P.sb("identf", [128, 128])
    P.dma(identf[:], id_d[:, :])
    mxf = P.sb("mxf", [128, 128])
    mzf = P.sb("mzf", [128, 256])
    ApT = [P.sb("ApT%d" % i, [128, 128], BF16) for i in range(2)]
    AV = [P.sb("AV%d" % i, [128, 128], BF16) for i in range(2)]
    Vp = [P.sb("Vp%d" % i, [128, 128]) for i in range(2)]
    Ub = [P.sb("Ub%d" % i, [128, 128], BF16) for i in range(2)]
    h64 = P.sb("h64", [128, 64])
    pA = P.ps("pA", [128, 512])
    pB = P.ps("pB", [128, 512])
    pC = P.ps("pC", [128, 512])
    pD = P.ps("pD", [128, 512])
    pE = P.ps("pE", [128, 512])
    pF = P.ps("pF", [128, 512])
    pY = [P.ps("pY%d" % i, [128, 512]) for i in range(2)]
    tpb = None

    def v3(t):
        return t[:, :].rearrange("p (h k) -> p h k", k=64)

    pT16 = [t[:, :].bitcast(BF16).rearrange("p (a t) -> p a t", t=128) for t in (pA, pB, pD, pE)]

    for d in range(ND):
        P.dma(tri[:], tri_d[d, :, :])
        P.dma(mx[:], mx_d[d, :, :], q="pool")
        P.dma(mz[:], mz_d[d, :, :], q="pool")
        P.dma(mxf[:], mx_d[d, :, 0:128])
        P.dma(mzf[:], mz_d[d, :, :])
        P.memset(H[:], 0.0)
        P.memset(Hb[:], 0.0)
        for ci, c in enumerate(ORDER[d][:NCK]):
            r0 = c * 128
            P.dma(rt[:], r_d[r0:r0 + 128, :], q="pool")
            P.dma(kt[:], k_d[r0:r0 + 128, :], q="pool")
            P.dma(vt[:], v_d[r0:r0 + 128, :], q="pool")
            P.dma(lw[:], lw_d[d][r0:r0 + 128, :])
            P.dma(ic[:], ic_d[d][r0:r0 + 128, :])
            if d == 1:
                P.dma(gt[:], g_d[r0:r0 + 128, :], q="pool")
                P.dma(yfl[:], yf_d[r0:r0 + 128, :])
                P.dma(bvl[:], bv_d[r0:r0 + 128, :])
            P.tt(f1[:], kt[:], KK_[:], ALU.mult)
            P.tt(f2[:], f1[:], f1[:], ALU.mult, eng="pool")
            P.op("dve", lambda o=s16, i=f2: P.nc.vector.tensor_reduce(out=o[:], in_=v3(i), op=ALU.add, axis=AX.X),
                 [f2[:]], [s16[:]])
            P.actv(s16[:], s16[:], AF.Sqrt)
            P.ts(s16[:], s16[:], 1e-12, ALU.max)
            P.recip(s16b[:], s16[:])
            P.tt(v3(kk), v3(f1), s16b[:, :].unsqueeze(2).broadcast_to([128, NHD, 64]), ALU.mult)
            if STAGE == 1:
                return P
            P.stt(f2[:], ic[:], -1.0, KA_[:], ALU.add, ALU.mult)
            P.stt(kd[:], f2[:], 1.0, kt[:], ALU.add, ALU.mult)
            P.tt(f3[:], kk[:], ic[:], ALU.mult, eng="pool")
            if STAGE == 2:
                return P
            for hb_ in range(2):
                pp = pA if hb_ == 0 else pB
                P.mm(pp[:, :], tri[:, :], lw[:, hb_ * 512:(hb_ + 1) * 512])
                P.actv(Ep[:, hb_ * 512:(hb_ + 1) * 512], pp[:, :], AF.Exp)
                P.actv(Em[:, hb_ * 512:(hb_ + 1) * 512], pp[:, :], AF.Exp, scale=-1.0)
                P.tt(f1[:, hb_ * 512:(hb_ + 1) * 512], pp[:, :], lw[:, hb_ * 512:(hb_ + 1) * 512], ALU.subtract)
            P.actv(Epx[:], f1[:], AF.Exp)
            if STAGE == 3:
                return P
            for p in range(NPAIR):
                P.mm(pC[:, 256 + p:256 + p + 1], lw[:, p * 128:(p + 1) * 128], ones[:, 0:1])
            P.actv(PC[:, :], pC[:, 256:256 + NPAIR], AF.Exp)
            if STAGE == 4:
                return P
            P.tt(RT_[:], rt[:], Ep[:], ALU.mult)
            P.stt(AT_[:], kk[:], -1.0, Epx[:], ALU.mult, ALU.mult)
            P.tt(BT_[:], f3[:], Em[:], ALU.mult)
            P.tt(KT_[:], kd[:], Em[:], ALU.mult)
            P.tt(f1[:], rt[:], kd[:], ALU.mult)
            P.tt(f1[:], f1[:], RK[:], ALU.mult, eng="pool")
            P.op("dve", lambda o=s16, i=f1: P.nc.vector.tensor_reduce(out=o[:], in_=v3(i), op=ALU.add, axis=AX.X),
                 [f1[:]], [s16[:]])
            P.tt(v3(f2), v3(vt), s16[:, :].unsqueeze(2).broadcast_to([128, NHD, 64]), ALU.mult)
            if d == 0:
                P.dma(bv_d[r0:r0 + 128, :], f2[:])
            else:
                P.tt(bvl[:], bvl[:], f2[:], ALU.add, eng="pool")
            if STAGE == 5:
                return P
            for qi, (src, dst) in enumerate(((AT_, 0), (RT_, 1), (BT_, 2), (KT_, 3))):
                for p in range(NPAIR):
                    P.tr(pT16[qi][:, p, :], src[:, p * 128:(p + 1) * 128], ident[:, :])
                src16 = pT16[qi]
                if dst == 0:
                    P.copy(ARf[:, :, 0, :], src16[:, :, :], eng="act")
                    P.copy(ATz[0:64, :, 0, :], src16[0:64, :, :], eng="dve")
                    P.copy(ATz[64:128, :, 1, :], src16[64:128, :, :], eng="act")
                elif dst == 1:
                    P.copy(ARf[:, :, 1, :], src16[:, :, :], eng="dve")
                elif dst == 2:
                    P.copy(BTf[:, :, :], src16[:, :, :], eng="act")
                    P.copy(BTz[0:64, :, 0, :], src16[0:64, :, :], eng="dve")
                    P.copy(BTz[64:128, :, 1, :], src16[64:128, :, :], eng="act")
                else:
                    P.copy(KTz[0:64, :, 0, :], src16[0:64, :, :], eng="dve")
                    P.copy(KTz[64:128, :, 1, :], src16[64:128, :, :], eng="act")
            if STAGE == 6:
                return P
            for p in range(NPL):
                b2 = p % 2
                sx, sy, sz = SX[b2], SY[b2], SZ[b2]
                for e in range(2):
                    P.mm(pA[:, e * 256:(e + 1) * 256], BTz[:, p, e, :], ARf[:, p, :, :].rearrange("p a t -> p (a t)"))
                    P.mm(pB[:, e * 256:(e + 1) * 256], KTz[:, p, e, :], ARf[:, p, :, :].rearrange("p a t -> p (a t)"))
                    P.mm(pC[:, e * 128:(e + 1) * 128], ATz[:, p, e, :], BTf[:, p, :])
                P.tt(sx[:], pA[:, :], mx[:], ALU.mult)
                P.tt(sy[:], pB[:, :], mx[:], ALU.mult)
                xy = XY[0]
                tt_ = Tt[b2]
                for e in range(2):
                    P.tt(xy[:, e * 128:(e + 1) * 128], pA[:, e * 256:e * 256 + 128], mxf[:, :], ALU.mult)
                P.tt(xy[:, 256:512], pC[:, 0:256], mzf[:, :], ALU.mult)
                for e in range(2):
                    P.tt(tt_[:, e * 128:(e + 1) * 128], xy[:, e * 128:(e + 1) * 128], identf[:, :], ALU.add, eng="pool")
                for lv in range(6):
                    xo = XY[lv % 2]
                    xn = XY[(lv + 1) % 2]
                    for e in range(2):
                        X_e = xo[:, e * 128:(e + 1) * 128]
                        Y_e = xo[:, 256 + e * 128:256 + (e + 1) * 128]
                        P.mm(pD[:, e * 128:(e + 1) * 128], Y_e, X_e)
                        P.mm(pD[:, 256 + e * 128:256 + (e + 1) * 128], X_e, Y_e)
                    P.copy(xn[:, :], pD[:, :], eng="act")
                    for e in range(2):
                        P.mm(pE[:, e * 128:(e + 1) * 128], xn[:, 256 + e * 128:256 + (e + 1) * 128],
                             tt_[:, e * 128:(e + 1) * 128])
                    P.tt(tt_[:, :], pE[:, 0:256], tt_[:, :], ALU.add)
                if STAGE == 8:
                    return P
                ttb = Ttb[b2]
                P.copy(ttb[:, :], tt_[:, :], eng="pool")
                apt, av, vp, ub = ApT[b2], AV[b2], Vp[b2], Ub[b2]
                for e in range(2):
                    hc = slice((2 * p + e) * 64, (2 * p + e + 1) * 64)
                    P.mm((pE[:, 256:384] if e == 0 else pF[:, 384:512]), AT_[:, p * 128:(p + 1) * 128],
                         ttb[:, e * 128:(e + 1) * 128])
                    P.mm(pF[:, e * 64:(e + 1) * 64], sy[:, e * 256:e * 256 + 128], vt[:, hc])
                P.copy(apt[0:64, :], pE[0:64, 256:384], eng="act")
                P.copy(apt[64:128, :], pF[64:128, 384:512], eng="act")
                P.copy(av[:, :], pF[:, 0:128], eng="act")
                for e in range(2):
                    P.mm(pF[:, 128 + e * 64:128 + (e + 1) * 64], ttb[:, e * 128:(e + 1) * 128], av[:, e * 64:(e + 1) * 64])
                P.copy(vp[:, :], pF[:, 128:256], eng="act")
                if STAGE == 9:
                    return P
                P.mm(pF[:, 256:384], apt[:, :], Hb[:, p, :, :].rearrange("p a v -> p (a v)"))
                P.tt(ub[:, :], pF[:, 256:384], vp[:, :], ALU.add)
                py = pY[p // 4]
                P.mm(py[:, (p % 4) * 128:(p % 4 + 1) * 128], ARf[:, p, 1, :],
                     Hb[:, p, :, :].rearrange("p a v -> p (a v)"), start=True, stop=False)
                for e in range(2):
                    hc = slice((2 * p + e) * 64, (2 * p + e + 1) * 64)
                    oc = slice((p % 4) * 128 + e * 64, (p % 4) * 128 + (e + 1) * 64)
                    P.mm(py[:, oc], sx[:, e * 256 + 128:e * 256 + 256], ub[:, e * 64:(e + 1) * 64], start=False, stop=False)
                    P.mm(py[:, oc], sy[:, e * 256 + 128:e * 256 + 256], vt[:, hc], start=False, stop=(e == 1))
                if STAGE == 10:
                    return P
                P.mm(pE[:, 384:512], BT_[:, p * 128:(p + 1) * 128], ub[:, :], start=True, stop=False)
                P.mm(pE[:, 384:512], KT_[:, p * 128:(p + 1) * 128], vt[:, p * 128:(p + 1) * 128], start=False, stop=True)
                P.tt(h64[0:64, :], pE[0:64, 384:448], H[0:64, p, :], ALU.add)
                P.tt(h64[64:128, :], pE[64:128, 448:512], H[64:128, p, :], ALU.add)
                P.ts(H[:, p, :], h64[:, :], PC[:, p:p + 1], ALU.mult)
                P.ts(Hb[0:64, p, 0, :], h64[0:64, :], PC[0:64, p:p + 1], ALU.mult)
                P.ts(Hb[64:128, p, 1, :], h64[64:128, :], PC[64:128, p:p + 1], ALU.mult)
            if STAGE == 11:
                return P
            if d == 0:
                for hb_ in range(2):
                    P.copy(ysb[:, hb_ * 512:(hb_ + 1) * 512], pY[hb_][:, :], eng="act")
                P.dma(yf_d[r0:r0 + 128, :], ysb[:])
            else:
                for hb_ in range(2):
                    P.tt(ysb[:, hb_ * 512:(hb_ + 1) * 512], pY[hb_][:, :], yfl[:, hb_ * 512:(hb_ + 1) * 512], ALU.add)
                if DEBUG:
                    P.dma(yt_d[r0:r0 + 128, :], ysb[:])
                P.op("dve", lambda o=s16, i=ysb: P.nc.vector.tensor_reduce(out=o[:], in_=v3(i), op=ALU.add, axis=AX.X),
                     [ysb[:]], [s16[:]])
                P.ts(s16[:], s16[:], 1.0 / 64, ALU.mult)
                P.tt(v3(f1), v3(ysb), s16[:, :].unsqueeze(2).broadcast_to([128, NHD, 64]), ALU.subtract)
                P.tt(f2[:], f1[:], f1[:], ALU.mult, eng="pool")
                P.op("dve", lambda o=s16, i=f2: P.nc.vector.tensor_reduce(out=o[:], in_=v3(i), op=ALU.add, axis=AX.X),
                     [f2[:]], [s16[:]])
                P.ts(s16[:], s16[:], 1.0 / 64, ALU.mult, GN_EPS, ALU.add)
                P.actv(s16[:], s16[:], AF.Sqrt)
                P.recip(s16b[:], s16[:])
                P.tt(v3(f1), v3(f1), s16b[:, :].unsqueeze(2).broadcast_to([128, NHD, 64]), ALU.mult)
                P.tt(f1[:], f1[:], GNW[:], ALU.mult)
                P.tt(f1[:], f1[:], GNB[:], ALU.add, eng="pool")
                P.tt(f1[:], f1[:], bvl[:], ALU.add, eng="pool")
                P.tt(f2[:], f1[:], gt[:], ALU.mult)
                P.dma(z_d[r0:r0 + 128, :], f2[:])
    return P

def rope_tables():
    rows = 4096 // 64
    row = np.repeat(np.arange(rows), 64).astype(np.float32)
    col = np.tile(np.arange(64), rows).astype(np.float32)
    inv = (np.float32(10000.0) ** (-np.arange(16, dtype=np.float32) / 16)).astype(np.float32)
    ar = row[:, None] * inv
    ac = col[:, None] * inv
    C = np.concatenate([np.cos(ar), np.cos(ar), np.cos(ac), np.cos(ac)], axis=1).astype(np.float32)
    S = np.concatenate([-np.sin(ar), np.sin(ar), -np.sin(ac), np.sin(ac)], axis=1).astype(np.float32)
    return np.ascontiguousarray(C.T), np.ascontiguousarray(S.T)

_PROGS = {}
_DBG = {}


def _get(name, fn):
    if name not in _PROGS:
        _PROGS[name] = fn().build()
    return _PROGS[name]


def _run(nc, in_maps):
    return run_bass_kernel_spmd(nc, in_maps, core_ids=list(range(8))).results


def _fm(v):
    v = np.asarray(v, np.float32)
    return np.ascontiguousarray(v.reshape(-1, KC, 128).transpose(2, 0, 1).reshape(128, -1))


def _pad(a):
    z = np.zeros((1, a.shape[1]), a.dtype)
    return np.concatenate([z, a, z], axis=0)


def _hm(s):
    h = np.zeros((128, 2), np.float32)
    h[:, 0] = 1.0 if s > 0 else 0.0
    h[:, 1] = 1.0 if s < 1 else 0.0
    return h


def _scan_masks():
    s = np.arange(128)[:, None]
    t = np.arange(128)[None, :]
    tri = np.stack([(s <= t), (s >= t)]).astype(np.float32)
    mst = np.stack([(s < t), (s > t)]).astype(np.float32)
    mi = np.stack([(s <= t), (s >= t)]).astype(np.float32)
    mx = np.ascontiguousarray(np.concatenate([mst, mi, mst, mi], axis=2))
    mstT = np.transpose(mst, (0, 2, 1))
    mz = np.ascontiguousarray(np.concatenate([mstT, mstT], axis=2))
    return tri, mx, mz


def kernel(x, c, ctx, c_ctx, ada_w, ada_b, norm_g, final_g,
           rk_mu, rk_wr, rk_wk, rk_wv, rk_wo, rk_w0, rk_w1, rk_w2, rk_a0, rk_a1, rk_a2,
           rk_g1, rk_g2, rk_kk, rk_ka, rk_rk, rk_gn_w, rk_gn_b,
           ml_wdown, ml_qnorm, ml_kvnorm, ml_wuq, ml_wukv, ml_wo,
           ff_wup, ff_conv, ff_convb, ff_wdown):
    A = lambda a: np.ascontiguousarray(np.asarray(a, np.float32))
    x, c, ctx, c_ctx = A(x), A(c), A(ctx), A(c_ctx)
    B = x.shape[0]
    ident = np.eye(128, dtype=np.float32)
    cores = [(b, s) for b in range(B) for s in range(2)]

    cc = np.zeros((8, D), np.float32)
    cc[:4] = c
    cc[4] = c_ctx
    cT = np.ascontiguousarray(cc.reshape(8, KC, 128).transpose(2, 1, 0).reshape(128, KC * 8))
    ada_w = np.asarray(ada_w, np.float32)
    ada_b = A(ada_b)
    maps = [{"cT_d": cT, "aw_d": np.ascontiguousarray(ada_w[:, :, q * ADA_W:(q + 1) * ADA_W]),
             "ab_d": np.ascontiguousarray(ada_b[:, q * ADA_W:(q + 1) * ADA_W])} for q in range(8)]
    res = _run(_get("ada", build_ada), maps)
    m_all = np.concatenate([r["m"] for r in res], axis=2)
    _DBG['m_all'] = m_all
    m_lat = [[np.ascontiguousarray(m_all[i, b].reshape(6, D)) for b in range(B)] for i in range(2)]
    m_ctx = [np.ascontiguousarray(m_all[i, 4].reshape(6, D)) for i in range(2)]

    norm_g = A(norm_g)
    shared = {"gT_d": _fm(norm_g[0, 0][None]), "mu_d": _fm(A(rk_mu)[0]), "wr_d": A(rk_wr)[0], "wk_d": A(rk_wk)[0],
              "wv_d": A(rk_wv)[0], "w1_d": A(rk_w1)[0], "w2_d": A(rk_w2)[0], "w0_d": A(rk_w0)[0],
              "a1_d": A(rk_a1)[0], "a2_d": A(rk_a2)[0], "a0_d": A(rk_a0)[0], "g1_d": A(rk_g1)[0],
              "g2_d": A(rk_g2)[0], "ident_d": ident}
    xpad = [_pad(x[b]) for b in range(B)]
    cpad = [_pad(ctx[b]) for b in range(B)]
    maps = []
    for (b, s) in cores:
        mp = dict(shared)
        mp["x0"] = np.ascontiguousarray(xpad[b][s * 2048:s * 2048 + 2050])
        mp["x1"] = np.ascontiguousarray(cpad[b][s * 128:s * 128 + 130])
        mp["modT0"] = _fm(m_lat[0][b])
        mp["modT1"] = _fm(m_ctx[0])
        mp["hmd0"] = _hm(s)
        mp["hmd1"] = _hm(s)
        maps.append(mp)
    res = _run(_get("rwkva", lambda: build_rwkva([2048, 128])), maps)

    def gather(key):
        out = []
        for b in range(B):
            r0, r1 = res[2 * b], res[2 * b + 1]
            out.append(np.concatenate([r0[key + "1"], r1[key + "1"], r0[key + "0"], r1[key + "0"]], axis=0))
        return out

    _pr0 = {k: gather(k) for k in ("or", "ok", "ov", "og", "olw0_", "olw1_", "oic0_", "oic1_")}
    pr = _pr0
    _DBG["pr"] = pr

    tri, mx, mz = _scan_masks()
    par_full = np.stack([A(rk_kk)[0], A(rk_ka)[0], A(rk_gn_w)[0], A(rk_gn_b)[0], A(rk_rk)[0].reshape(-1)])
    maps = []
    for (b, j) in cores:
        cs = slice(j * 1024, (j + 1) * 1024)
        sl = lambda a: np.ascontiguousarray(a[:, cs])
        maps.append({"r_d": sl(pr["or"][b]), "k_d": sl(pr["ok"][b]), "v_d": sl(pr["ov"][b]), "g_d": sl(pr["og"][b]),
                     "lwd0": sl(pr["olw0_"][b]), "lwd1": sl(pr["olw1_"][b]), "icd0": sl(pr["oic0_"][b]),
                     "icd1": sl(pr["oic1_"][b]), "par_d": sl(par_full), "tri_d": tri, "maskx": mx, "maskz": mz,
                     "ident_d": ident})
    res = _run(_get("rwkvb", build_rwkvb), maps)
    z_full = [np.concatenate([res[2 * b]["z"], res[2 * b + 1]["z"]], axis=1) for b in range(B)]

    _DBG['z_full'] = z_full
    def ffn_shared(i, wo):
        conv = A(ff_conv)[i]
        return {"wo_d": A(wo), "g2": np.ascontiguousarray(norm_g[i, 1][None]), "fg": A(final_g)[None],
                "wup": A(ff_wup)[i],
                "cw_d": np.ascontiguousarray(conv.T.reshape(NFC, 128, 3).transpose(1, 0, 2).reshape(128, NFC * 3)),
                "cb_d": np.ascontiguousarray(A(ff_convb)[i].reshape(NFC, 128).T), "wdn": A(ff_wdown)[i],
                "ident_d": ident}

    shared = ffn_shared(0, A(rk_wo)[0])
    zlp = [_pad(z_full[b][256:]) for b in range(B)]
    zcp = [_pad(z_full[b][:256]) for b in range(B)]
    maps = []
    for (b, s) in cores:
        mp = dict(shared)
        mp["x0"] = np.ascontiguousarray(xpad[b][s * 2048:s * 2048 + 2050])
        mp["z0"] = np.ascontiguousarray(zlp[b][s * 2048:s * 2048 + 2050])
        mp["x1"] = np.ascontiguousarray(cpad[b][s * 128:s * 128 + 130])
        mp["z1"] = np.ascontiguousarray(zcp[b][s * 128:s * 128 + 130])
        mp["mod0"] = m_lat[0][b]
        mp["mod1"] = m_ctx[0]
        mp["hmd0"] = _hm(s)
        mp["hmd1"] = _hm(s)
        maps.append(mp)
    res = _run(_get("ffn0", lambda: build_ffn([2048, 128], False)), maps)
    x1 = [np.concatenate([res[2 * b]["y0"], res[2 * b + 1]["y0"]], axis=0) for b in range(B)]
    sc1 = [np.concatenate([res[2 * b]["y1"], res[2 * b + 1]["y1"]], axis=0) for b in range(B)]

    _DBG['x1'] = x1
    _DBG['sc1'] = sc1
    C, S = rope_tables()
    ctk = np.ascontiguousarray(np.concatenate([np.ones((64, NCTX), np.float32), C], axis=1))
    stk = np.ascontiguousarray(np.concatenate([np.zeros((64, NCTX), np.float32), S], axis=1))
    shared = {"g1": np.ascontiguousarray(norm_g[1, 0][None]), "wdown": A(ml_wdown)[0], "qn": A(ml_qnorm)[0][None],
              "kvn": A(ml_kvnorm)[0][None], "wuq_d": A(ml_wuq)[0], "wukv_d": A(ml_wukv)[0], "ctk_d": ctk,
              "stk_d": stk, "ident_d": ident, "modc": m_ctx[1]}
    maps = []
    for (b, s) in cores:
        mp = dict(shared)
        mp["xk"] = np.ascontiguousarray(np.concatenate([sc1[b], x1[b]], axis=0))
        mp["xq"] = np.ascontiguousarray(x1[b][s * 2048:(s + 1) * 2048])
        mp["modl"] = m_lat[1][b]
        mp["ctq_d"] = np.ascontiguousarray(C[:, s * 2048:(s + 1) * 2048])
        mp["stq_d"] = np.ascontiguousarray(S[:, s * 2048:(s + 1) * 2048])
        maps.append(mp)
    res = _run(_get("mla", build_mla), maps)
    o_full = [np.concatenate([res[2 * b]["o"], res[2 * b + 1]["o"]], axis=0) for b in range(B)]

    _DBG['o_full'] = o_full
    shared = ffn_shared(1, A(ml_wo)[0])
    maps = []
    for (b, s) in cores:
        mp = dict(shared)
        mp["x0"] = np.ascontiguousarray(_pad(x1[b])[s * 2048:s * 2048 + 2050])
        mp["z0"] = np.ascontiguousarray(_pad(o_full[b])[s * 2048:s * 2048 + 2050])
        mp["mod0"] = m_lat[1][b]
        mp["hmd0"] = _hm(s)
        maps.append(mp)
    res = _run(_get("ffn1", lambda: build_ffn([2048], True)), maps)
    out = np.stack([np.concatenate([res[2 * b]["y0"], res[2 * b + 1]["y0"]], axis=0) for b in range(B)])
    return out.astype(np.float32)
```
